# Optimizing a Trainium2 kernel written in Bass

```python
import jax, jax.numpy as jnp
from jax import lax
import numpy as np

D_MODEL = 1024
BATCH = 1
SEQ = 16384
DEPTH = 2

HEAD_DIM = 64
A_HEADS = 8
A_KV = 2
B_HEADS = 8
B_KV = 2
C_HEADS = 16
D_FF = 2816
GRID_W = 64
Q_BLOCK = 128
WINDOW = 128
NA_KH = 8
NA_KW = 16
ROPE_THETA = 10000.0
EPS = 1e-6
EVEN_IN = (A_HEADS + 2 * A_KV + B_HEADS + 2 * B_KV) * HEAD_DIM
EVEN_OUT = (A_HEADS + B_HEADS) * HEAD_DIM
ODD_IN = 3 * C_HEADS * HEAD_DIM
ODD_OUT = C_HEADS * HEAD_DIM
NEG_INF = -1e30

kernel_name = "hybrid_axial_window_neighbourhood_encoder"


def rms_norm(x, g):
    xf = x.astype(jnp.float32)
    y = xf * lax.rsqrt(jnp.mean(xf * xf, axis=-1, keepdims=True) + EPS)
    return (y * g.astype(jnp.float32)).astype(x.dtype)


def swiglu(h, w_gate, w_up, w_down):
    return (jax.nn.silu(h @ w_gate) * (h @ w_up)) @ w_down


def rope_cos_sin(pos, dim):
    inv = ROPE_THETA ** (-jnp.arange(0, dim, 2, dtype=jnp.float32) / dim)
    ang = pos.astype(jnp.float32)[:, None] * inv[None, :]
    return jnp.cos(ang), jnp.sin(ang)


def apply_rope(x, cos, sin):
    half = x.shape[-1] // 2
    c = cos[None, :, None, :].astype(x.dtype)
    s = sin[None, :, None, :].astype(x.dtype)
    x1, x2 = x[..., :half], x[..., half:]
    return jnp.concatenate([x1 * c - x2 * s, x1 * s + x2 * c], axis=-1)


def apply_axial_rope(x, cos_r, sin_r, cos_c, sin_c):
    half = x.shape[-1] // 2
    return jnp.concatenate([apply_rope(x[..., :half], cos_r, sin_r),
                            apply_rope(x[..., half:], cos_c, sin_c)], axis=-1)


def global_gqa(q, k, v):
    bsz, s_len, h, d = q.shape
    kv = k.shape[2]
    g = h // kv
    nb = s_len // Q_BLOCK
    scale = HEAD_DIM ** -0.5
    qb = q.reshape(bsz, nb, Q_BLOCK, kv, g, d).transpose(1, 0, 2, 3, 4, 5)

    def block(qi):
        s = jnp.einsum('bqkgd,bskd->bkgqs', qi, k, preferred_element_type=jnp.float32) * scale
        p = jax.nn.softmax(s, axis=-1).astype(v.dtype)
        return jnp.einsum('bkgqs,bskd->bqkgd', p, v)

    o = lax.map(block, qb)
    return o.transpose(1, 0, 2, 3, 4, 5).reshape(bsz, s_len, h * d)


def windowed_gqa_sink(q, k, v, sink):
    bsz, s_len, h, d = q.shape
    kv = k.shape[2]
    g = h // kv
    nb = s_len // Q_BLOCK
    scale = HEAD_DIM ** -0.5
    qb = q.reshape(bsz, nb, Q_BLOCK, kv, g, d)
    pad = ((0, 0), (Q_BLOCK, Q_BLOCK), (0, 0), (0, 0))
    kb = jnp.pad(k, pad).reshape(bsz, nb + 2, Q_BLOCK, kv, d)
    vb = jnp.pad(v, pad).reshape(bsz, nb + 2, Q_BLOCK, kv, d)
    k_slab = jnp.concatenate([kb[:, :-2], kb[:, 1:-1], kb[:, 2:]], axis=2)
    v_slab = jnp.concatenate([vb[:, :-2], vb[:, 1:-1], vb[:, 2:]], axis=2)
    s = jnp.einsum('bnqkgd,bnskd->bnkgqs', qb, k_slab, preferred_element_type=jnp.float32) * scale
    blk = jnp.arange(nb)[:, None, None] * Q_BLOCK
    qpos = blk + jnp.arange(Q_BLOCK)[None, :, None]
    kpos = blk - Q_BLOCK + jnp.arange(3 * Q_BLOCK)[None, None, :]
    valid = (jnp.abs(qpos - kpos) <= WINDOW) & (kpos >= 0) & (kpos < s_len)
    s = jnp.where(valid[None, :, None, None], s, NEG_INF)
    sk = sink.astype(jnp.float32).reshape(kv, g)[None, None, :, :, None]
    m = jnp.maximum(jnp.max(s, axis=-1), sk)
    p = jnp.exp(s - m[..., None])
    denom = jnp.sum(p, axis=-1) + jnp.exp(sk - m)
    p = (p / denom[..., None]).astype(v.dtype)
    o = jnp.einsum('bnkgqs,bnskd->bnqkgd', p, v_slab)
    return o.reshape(bsz, s_len, h * d)


def neighbourhood_attention(q, k, v, rel_bias, rows):
    bsz, s_len, h, d = q.shape
    kh = min(NA_KH, rows)
    kw = NA_KW
    scale = HEAD_DIM ** -0.5
    qg = q.reshape(bsz, rows, GRID_W, h, d)
    kg = k.reshape(bsz, rows, GRID_W, h, d)
    vg = v.reshape(bsz, rows, GRID_W, h, d)
    cols = jnp.arange(GRID_W)
    col_start = jnp.clip(cols - kw // 2, 0, GRID_W - kw)
    col_idx = col_start[:, None] + jnp.arange(kw)[None, :]
    col_bias_idx = col_idx - cols[:, None] + (NA_KW - 1)

    def row_fn(args):
        q_r, r = args
        rs = jnp.clip(r - kh // 2, 0, rows - kh)
        k_rows = lax.dynamic_slice_in_dim(kg, rs, kh, axis=1)
        v_rows = lax.dynamic_slice_in_dim(vg, rs, kh, axis=1)
        k_win = k_rows[:, :, col_idx]
        v_win = v_rows[:, :, col_idx]
        row_bias_idx = rs + jnp.arange(kh) - r + (NA_KH - 1)
        bias = rel_bias[:, row_bias_idx[:, None, None], col_bias_idx[None, :, :]]
        s = jnp.einsum('bchd,bacwhd->bhcaw', q_r, k_win, preferred_element_type=jnp.float32) * scale
        s = s + bias.transpose(0, 2, 1, 3)[None].astype(jnp.float32)
        p = jax.nn.softmax(s.reshape(bsz, h, GRID_W, kh * kw), axis=-1)
        p = p.reshape(s.shape).astype(v.dtype)
        return jnp.einsum('bhcaw,bacwhd->bchd', p, v_win)

    o = lax.map(row_fn, (qg.transpose(1, 0, 2, 3, 4), jnp.arange(rows)))
    return o.transpose(1, 0, 2, 3, 4).reshape(bsz, s_len, h * d)


def even_mixer(h, w_in, q_gain, k_gain, sink, w_out, rope1, rope_axial):
    bsz, s_len, _ = h.shape
    proj = h @ w_in
    sizes = [A_HEADS * HEAD_DIM, A_KV * HEAD_DIM, A_KV * HEAD_DIM,
             B_HEADS * HEAD_DIM, B_KV * HEAD_DIM, B_KV * HEAD_DIM]
    cuts = list(np.cumsum(sizes)[:-1])
    qa, ka, va, qb, kb, vb = jnp.split(proj, cuts, axis=-1)
    hd = lambda t: t.reshape(bsz, s_len, -1, HEAD_DIM)
    qa, ka, va, qb, kb, vb = map(hd, (qa, ka, va, qb, kb, vb))
    qa = apply_axial_rope(rms_norm(qa, q_gain), *rope_axial)
    ka = apply_axial_rope(rms_norm(ka, k_gain), *rope_axial)
    oa = global_gqa(qa, ka, va)
    qb = apply_rope(qb, *rope1)
    kb = apply_rope(kb, *rope1)
    ob = windowed_gqa_sink(qb, kb, vb, sink)
    return jnp.concatenate([oa, ob], axis=-1) @ w_out


def odd_mixer(h, w_qkv, rel_bias, w_out, rows):
    bsz, s_len, _ = h.shape
    q, k, v = jnp.split(h @ w_qkv, 3, axis=-1)
    q = q.reshape(bsz, s_len, C_HEADS, HEAD_DIM)
    k = k.reshape(bsz, s_len, C_HEADS, HEAD_DIM)
    v = v.reshape(bsz, s_len, C_HEADS, HEAD_DIM)
    return neighbourhood_attention(q, k, v, rel_bias, rows) @ w_out


def setup_inputs(seed: int = 0) -> dict:
    key = jax.random.key(seed)
    ks = iter(jax.random.split(key, 32))
    f32 = jnp.float32
    n_even = (DEPTH + 1) // 2
    n_odd = DEPTH // 2

    def w(shape, fan_in):
        return jax.random.normal(next(ks), shape, f32) * (fan_in ** -0.5)

    def gain(shape):
        return 1.0 + 0.1 * jax.random.normal(next(ks), shape, f32)

    return {
        "x": jax.random.normal(next(ks), (BATCH, SEQ, D_MODEL), f32),
        "ffn1_norm": gain((DEPTH, D_MODEL)),
        "ffn1_w_gate": w((DEPTH, D_MODEL, D_FF), D_MODEL),
        "ffn1_w_up": w((DEPTH, D_MODEL, D_FF), D_MODEL),
        "ffn1_w_down": w((DEPTH, D_FF, D_MODEL), D_FF),
        "mix_norm": gain((DEPTH, D_MODEL)),
        "ffn2_norm": gain((DEPTH, D_MODEL)),
        "ffn2_w_gate": w((DEPTH, D_MODEL, D_FF), D_MODEL),
        "ffn2_w_up": w((DEPTH, D_MODEL, D_FF), D_MODEL),
        "ffn2_w_down": w((DEPTH, D_FF, D_MODEL), D_FF),
        "even_w_in": w((n_even, D_MODEL, EVEN_IN), D_MODEL),
        "a_q_norm": gain((n_even, HEAD_DIM)),
        "a_k_norm": gain((n_even, HEAD_DIM)),
        "b_sink": 0.5 * jax.random.normal(next(ks), (n_even, B_HEADS), f32),
        "even_w_out": w((n_even, EVEN_OUT, D_MODEL), EVEN_OUT),
        "odd_w_qkv": w((n_odd, D_MODEL, ODD_IN), D_MODEL),
        "c_rel_bias": 0.1 * jax.random.normal(next(ks), (n_odd, C_HEADS, 2 * NA_KH - 1, 2 * NA_KW - 1), f32),
        "odd_w_out": w((n_odd, ODD_OUT, D_MODEL), ODD_OUT),
        "final_norm": gain((D_MODEL,)),
    }


def reference(x, ffn1_norm, ffn1_w_gate, ffn1_w_up, ffn1_w_down, mix_norm,
              ffn2_norm, ffn2_w_gate, ffn2_w_up, ffn2_w_down,
              even_w_in, a_q_norm, a_k_norm, b_sink, even_w_out,
              odd_w_qkv, c_rel_bias, odd_w_out, final_norm):
    s_len = x.shape[1]
    rows = s_len // GRID_W
    pos = jnp.arange(s_len)
    rope1 = rope_cos_sin(pos, HEAD_DIM)
    cos_r, sin_r = rope_cos_sin(pos // GRID_W, HEAD_DIM // 2)
    cos_c, sin_c = rope_cos_sin(pos % GRID_W, HEAD_DIM // 2)
    rope_axial = (cos_r, sin_r, cos_c, sin_c)
    for layer in range(DEPTH):
        i = layer // 2
        h = rms_norm(x, ffn1_norm[layer])
        x = x + 0.5 * swiglu(h, ffn1_w_gate[layer], ffn1_w_up[layer], ffn1_w_down[layer])
        h = rms_norm(x, mix_norm[layer])
        if layer % 2 == 0:
            x = x + even_mixer(h, even_w_in[i], a_q_norm[i], a_k_norm[i], b_sink[i],
                               even_w_out[i], rope1, rope_axial)
        else:
            x = x + odd_mixer(h, odd_w_qkv[i], c_rel_bias[i], odd_w_out[i], rows)
        h = rms_norm(x, ffn2_norm[layer])
        x = x + 0.5 * swiglu(h, ffn2_w_gate[layer], ffn2_w_up[layer], ffn2_w_down[layer])
    return rms_norm(x, final_norm)
```

```python
import contextlib
import numpy as np
import concourse.bass as bass
import concourse.mybir as mybir
from concourse.bass_utils import run_bass_kernel_spmd

F32 = mybir.dt.float32
BF16 = mybir.dt.bfloat16
AF = mybir.ActivationFunctionType
ALU = mybir.AluOpType
AX = mybir.AxisListType

NCORES = 8
T = 2048
NT = 16
D = 1024
DFF = 2816
NFC = 22
NEG = -30000.0
EPS = 1e-6
SCALE = 0.125
ENGS = ("pe", "act", "dve", "pool", "sp")


class Op:
    __slots__ = ("eng", "fn", "deps", "sem", "val", "vc", "is_dma", "needed")

    def __init__(self, eng, fn, is_dma):
        self.eng = eng
        self.fn = fn
        self.deps = []
        self.sem = None
        self.val = 0
        self.vc = None
        self.is_dma = is_dma
        self.needed = False


class Sched:
    def __init__(self, nc):
        self.nc = nc
        self.ops = {e: [] for e in ENGS}
        self.last_w = {}
        self.last_r = {}
        self.barrier_ops = None
        self.seen_after_barrier = set()
        self.all_last = {}
        self.keymap = {}

    def add(self, eng, fn, reads=(), writes=(), dma_key=None):
        is_dma = dma_key is not None
        op = Op(eng, fn, is_dma)
        if is_dma:
            if dma_key not in self.keymap:
                self.keymap[dma_key] = len(self.keymap)
            dma_key = self.keymap[dma_key]
        sk = ("dma", dma_key) if is_dma else ("eng", eng)
        op.sem = sk
        deps = {}
        for r in reads:
            for o in self.last_w.get(r, {}).values():
                deps[id(o)] = o
        for w in writes:
            for o in self.last_w.get(w, {}).values():
                deps[id(o)] = o
            for o in self.last_r.get(w, {}).values():
                deps[id(o)] = o
        if self.barrier_ops is not None and eng not in self.seen_after_barrier:
            self.seen_after_barrier.add(eng)
            for o in self.barrier_ops:
                deps[id(o)] = o
        op.deps = list(deps.values())
        for r in reads:
            self.last_r.setdefault(r, {})[sk] = op
        for w in writes:
            self.last_w[w] = {sk: op}
            self.last_r[w] = {}
        self.ops[eng].append(op)
        self.all_last[sk] = op
        return op

    def barrier(self):
        self.barrier_ops = list(self.all_last.values())
        self.seen_after_barrier = set()
        self.keymap = {}

    def emit(self, final_wait_ops=()):
        nc = self.nc
        for e in ENGS:
            for op in self.ops[e]:
                for d in op.deps:
                    if d.is_dma:
                        d.needed = True
                    elif d.eng == "pe" and op.eng == "pe" and not op.is_dma:
                        pass
                    else:
                        d.needed = True
        for o in final_wait_ops:
            o.needed = True
        cnt = {}
        for e in ENGS:
            for op in self.ops[e]:
                if op.is_dma:
                    cnt[op.sem] = cnt.get(op.sem, 0) + 16
                    op.val = cnt[op.sem]
                elif op.needed:
                    cnt[op.sem] = cnt.get(op.sem, 0) + 1
                    op.val = cnt[op.sem]
        semkeys = sorted(cnt.keys(), key=str)
        assert len(semkeys) < 220, len(semkeys)

        def vc_of(op):
            if op.vc is not None:
                return op.vc
            stack = [op]
            while stack:
                o = stack[-1]
                if o.vc is not None:
                    stack.pop()
                    continue
                pend = [d for d in o.deps if d.vc is None]
                if pend:
                    stack.extend(pend)
                    continue
                v = {}
                for d in o.deps:
                    for k, x in d.vc.items():
                        if v.get(k, 0) < x:
                            v[k] = x
                if o.val and v.get(o.sem, 0) < o.val:
                    v[o.sem] = o.val
                o.vc = v
                stack.pop()
            return op.vc

        self.nwaits = 0
        with contextlib.ExitStack() as st:
            sems = {}
            for i, k in enumerate(semkeys):
                sems[k] = st.enter_context(nc.semaphore("s%d" % i))
            block = st.enter_context(nc.Block())
            handles = {"pe": block.tensor, "act": block.scalar, "dve": block.vector,
                       "pool": block.gpsimd, "sp": block.sync}
            for e in ENGS:
                ops = self.ops[e]
                if not ops and not (e == "sp" and final_wait_ops):
                    continue

                def body(eng, ops=ops, e=e):
                    known = {}
                    for op in ops:
                        for d in op.deps:
                            if not d.val:
                                continue
                            if known.get(d.sem, 0) >= d.val:
                                continue
                            eng.wait_ge(sems[d.sem], d.val)
                            self.nwaits += 1
                            for k, x in vc_of(d).items():
                                if known.get(k, 0) < x:
                                    known[k] = x
                        ins = op.fn(eng)
                        if op.is_dma:
                            ins.then_inc(sems[op.sem], 16)
                        elif op.needed:
                            ins.then_inc(sems[op.sem], 1)
                    if e == "sp":
                        for o in final_wait_ops:
                            if known.get(o.sem, 0) < o.val:
                                eng.wait_ge(sems[o.sem], o.val)
                                known[o.sem] = o.val

                handles[e](body)


class Arena:
    _n = 0

    def __init__(self, nc, base, limit):
        self.nc = nc
        self.base = base
        self.limit = limit
        self.cur = base

    def alloc(self, name, shape, dtype):
        esz = 4 if dtype == F32 else 2
        n = esz
        for s in shape[1:]:
            n *= s
        n = (n + 63) // 64 * 64
        assert self.cur + n <= self.limit, (name, self.cur, n, self.limit)
        Arena._n += 1
        t = self.nc.alloc_sbuf_tensor_at("%s_%d" % (name, Arena._n), list(shape), dtype, offset=self.cur)
        self.cur += n
        return t.ap()

    def mark(self):
        return self.cur

    def reset(self, to):
        self.cur = to


class Builder:
    def __init__(self, stop_after=None, skip=()):
        self.stop_after = stop_after
        self.skip = set(skip)
        nc = bass.Bass("TRN2", target_bir_lowering=False)
        self.nc = nc
        self.S = Sched(nc)
        di = lambda name, shape, dt=F32: nc.dram_tensor(name, list(shape), dt, kind="ExternalInput").ap()
        self.x_in = di("x", [T, D])
        self.gains = di("gains", [7, D])
        fs = [2, D, DFF] if "ffn" not in self.skip else [2, 128, 128]
        self.wg = [di("wg%d" % i, fs) for i in (1, 2)]
        self.wu = [di("wu%d" % i, fs) for i in (1, 2)]
        self.wd = [di("wd%d" % i, [fs[0], fs[2], fs[1]]) for i in (1, 2)]
        self.w_in = di("w_in", [D, 1536])
        self.w_out0 = di("w_out0", [D, D])
        self.w_qkv = di("w_qkv", [D, 3072])
        self.w_out1 = di("w_out1", [D, D])
        self.ga_in = di("ga", [1, 640])
        self.sink_in = di("sinkrow", [1, 1024])
        self.rope_in = di("rope", [T, 256])
        self.mt_in = di("mt", [4, 128, 7 * 512])
        self.rowbias_in = di("rowbias", [128, 16 * 7 * 2])
        self.bedge_in = di("bedge", [128, 2])
        self.out = nc.dram_tensor("out", [T, D], F32, kind="ExternalOutput").ap()
        self.xs_dram = nc.dram_tensor("xs_dram", [128, NT * D], F32).ap()
        self.WA = 4672
        self.mineA = nc.dram_tensor("mineA", [128, self.WA], BF16)
        self.gathA = nc.dram_tensor("gathA", [NCORES * 128, self.WA], BF16)
        self.WC = 8320
        self.mineC = nc.dram_tensor("mineC", [128, self.WC], BF16)
        self.gathC = nc.dram_tensor("gathC", [NCORES * 128, self.WC], BF16)
        self.dmy_in = nc.dram_tensor("dmy_in", [128, 64], BF16)
        self.dmy_out = [nc.dram_tensor("dmy_out%d" % i, [NCORES * 128, 64], BF16) for i in range(2)]
        self.qc_dram = nc.dram_tensor("qc_dram", [8, 128, T], BF16).ap()
        self.kc_dram = nc.dram_tensor("kc_dram", [8, 128, T], BF16).ap()
        self.vc_dram = nc.dram_tensor("vc_dram", [128, NT, 16 * 65], BF16).ap()

        base = (nc._sbuf_addr_for_side("left") + 63) // 64 * 64
        total = nc._sbuf_addr_for_side("right") // 64 * 64
        self.total = total
        ar = Arena(nc, base, total)
        self.ident = ar.alloc("ident", [128, 128], BF16)
        self.identf = ar.alloc("identf", [128, 128], F32)
        self.ones_f = ar.alloc("ones_f", [128, 64], F32)
        self.Gb = ar.alloc("Gb", [128, D], F32)
        self.ss = ar.alloc("ss", [128, NT], F32)
        self.rstd = ar.alloc("rstd", [128, NT], F32)
        self.junk = ar.alloc("junk", [128, D], BF16)
        self.hn = [ar.alloc("hn", [128, D], BF16) for _ in range(2)]
        self.HT = ar.alloc("HT", [128, 8, T], BF16)
        self.X = ar.alloc("X", [128, NT, D], F32)
        self.x_base = ar.mark() - NT * D * 4
        self.free_base = ar.mark()
        self.ar = ar
        ps = nc.alloc_psum_tensor("ps", [128, 4096], F32).ap()
        self.ps = ps
        self.psb = ps.bitcast(BF16)
        self.cnt = {}

    def bank(self, i, n=1):
        return self.ps[:, i * 512:(i + n) * 512]

    def bankb(self, i, n=1):
        return self.psb[:, i * 1024:(i + n) * 1024]

    def rrow(self, e, which):
        if getattr(self, "_rrow", None) is None:
            pid = e.partition_id()
            self._rrow = [((pid + NCORES - 1) % NCORES) * 128, ((pid + 1) % NCORES) * 128]
        return self._rrow[which]

    def nxt(self, key, mod):
        v = self.cnt.get(key, 0)
        self.cnt[key] = v + 1
        return v % mod

    def consts(self):
        S = self.S
        S.add("pool", lambda e: e.memset(self.identf, 0.0), writes=["identf"])
        S.add("pool", lambda e: e.affine_select(out=self.identf, in_=self.identf, pattern=[[-1, 128]],
                                                compare_op=ALU.not_equal, fill=1.0, base=0, channel_multiplier=1),
              reads=["identf"], writes=["identf"])
        S.add("dve", lambda e: e.tensor_copy(out=self.ident, in_=self.identf), reads=["identf"], writes=["ident"])
        S.add("pool", lambda e: e.memset(self.ones_f, 1.0), writes=["ones_f"])

    def rstd_all(self):
        S = self.S
        for t in range(NT):
            S.add("act", lambda e, t=t: e.activation(out=self.junk, in_=self.X[:, t, :], func=AF.Square,
                                                     accum_out=self.ss[:, t:t + 1]),
                  reads=[("X", t)], writes=[("ss", t)])
        S.add("act", lambda e: e.activation(out=self.rstd, in_=self.ss, func=AF.Sqrt, scale=1.0 / D, bias=EPS),
              reads=[("ss", t) for t in range(NT)], writes=["rstd_tmp"])
        S.add("dve", lambda e: e.reciprocal(out=self.rstd, in_=self.rstd), reads=["rstd_tmp"],
              writes=[("rstd", t) for t in range(NT)] + ["rstd_tmp"])

    def norm_to_HT(self, gidx):
        S = self.S
        S.add("sp", lambda e: e.dma_start(out=self.Gb, in_=self.gains[gidx:gidx + 1, :].broadcast_to([128, D])),
              writes=["Gb"], dma_key="Gb")
        self.rstd_all()
        for t in range(NT):
            hb = self.nxt("hn", 2)
            S.add("dve", lambda e, t=t, hb=hb: e.scalar_tensor_tensor(out=self.hn[hb], in0=self.X[:, t, :],
                                                                       scalar=self.rstd[:, t:t + 1], in1=self.Gb,
                                                                       op0=ALU.mult, op1=ALU.mult),
                  reads=[("X", t), ("rstd", t), "Gb"], writes=[("hn", hb)])
            pb = 6 + self.nxt("normps", 2)
            for kc in range(8):
                S.add("pe", lambda e, kc=kc, hb=hb, pb=pb: e.transpose(
                    out=self.bankb(pb)[:, kc * 128:(kc + 1) * 128], in_=self.hn[hb][:, kc * 128:(kc + 1) * 128],
                    identity=self.ident), reads=[("hn", hb), "ident"], writes=[("ps", pb)])
            S.add("act", lambda e, t=t, pb=pb: e.activation(
                out=self.HT[:, :, t * 128:(t + 1) * 128],
                in_=self.bankb(pb).rearrange("p (k n) -> p k n", k=8), func=AF.Copy),
                reads=[("ps", pb)], writes=[("HT", t)])

    def ffn(self, wg, wu, wd):
        S = self.S
        ar = self.ar
        ar.reset(self.free_base)
        ar.limit = self.total
        CM = 6
        groups = [(0, 6), (6, 12), (12, 17), (17, 22)]
        actT = ar.alloc("actT", [128, CM, T], BF16)
        wd_bf = [ar.alloc("wd_bf", [128, CM, D], BF16) for _ in range(2)]
        wg_bf = [ar.alloc("wg_bf", [128, 8, 128], BF16) for _ in range(2)]
        wu_bf = [ar.alloc("wu_bf", [128, 8, 128], BF16) for _ in range(2)]
        stg_g = [ar.alloc("stg_g", [128, 8, 128], F32) for _ in range(2)]
        stg_u = [ar.alloc("stg_u", [128, 8, 128], F32) for _ in range(2)]
        stg_d = [ar.alloc("stg_d", [128, D], F32) for _ in range(2)]
        sg = [ar.alloc("sg", [128, 512], F32) for _ in range(2)]
        wg_v = wg.rearrange("(kc p) n -> p kc n", p=128)
        wu_v = wu.rearrange("(kc p) n -> p kc n", p=128)
        wd_v = wd.rearrange("(c p) n -> p c n", p=128)
        for gi, (c0, c1) in enumerate(groups):
            C = c1 - c0
            gb = gi % 2
            for ci in range(C):
                c = c0 + ci
                sb = self.nxt("stg_d", 2)
                S.add("sp", lambda e, sb=sb, c=c: e.dma_start(out=stg_d[sb], in_=wd_v[:, c, :]),
                      writes=[("stg_d", sb)], dma_key="stg_d%d" % sb)
                S.add("pool", lambda e, sb=sb, ci=ci, gb=gb: e.tensor_copy(out=wd_bf[gb][:, ci, :], in_=stg_d[sb]),
                      reads=[("stg_d", sb)], writes=[("wd_bf", gb, ci)])
            for ci in range(C):
                c = c0 + ci
                wb = self.nxt("gu", 2)
                S.add("sp", lambda e, wb=wb, c=c: e.dma_start(out=stg_g[wb], in_=wg_v[:, :, c * 128:(c + 1) * 128]),
                      writes=[("stg_g", wb)], dma_key="stg_g%d" % wb)
                S.add("sp", lambda e, wb=wb, c=c: e.dma_start(out=stg_u[wb], in_=wu_v[:, :, c * 128:(c + 1) * 128]),
                      writes=[("stg_u", wb)], dma_key="stg_u%d" % wb)
                S.add("pool", lambda e, wb=wb: e.tensor_copy(out=wg_bf[wb], in_=stg_g[wb]),
                      reads=[("stg_g", wb)], writes=[("wg_bf", wb)])
                S.add("pool", lambda e, wb=wb: e.tensor_copy(out=wu_bf[wb], in_=stg_u[wb]),
                      reads=[("stg_u", wb)], writes=[("wu_bf", wb)])
                for b in range(4):
                    pgi = self.nxt("ffn_pg", 2)
                    pui = 2 + pgi
                    htr = [("HT", 4 * b + i) for i in range(4)]
                    for kc in range(8):
                        S.add("pe", lambda e, kc=kc, wb=wb, b=b, pgi=pgi: e.matmul(
                            self.bank(pgi), lhsT=wg_bf[wb][:, kc, :], rhs=self.HT[:, kc, b * 512:(b + 1) * 512],
                            start=(kc == 0), stop=(kc == 7)), reads=[("wg_bf", wb)] + htr, writes=[("ps", pgi)])
                    for kc in range(8):
                        S.add("pe", lambda e, kc=kc, wb=wb, b=b, pui=pui: e.matmul(
                            self.bank(pui), lhsT=wu_bf[wb][:, kc, :], rhs=self.HT[:, kc, b * 512:(b + 1) * 512],
                            start=(kc == 0), stop=(kc == 7)), reads=[("wu_bf", wb)] + htr, writes=[("ps", pui)])
                    sgb = self.nxt("sg", 2)
                    S.add("act", lambda e, sgb=sgb, pgi=pgi: e.activation(out=sg[sgb], in_=self.bank(pgi), func=AF.Silu),
                          reads=[("ps", pgi)], writes=[("sg", sgb)])
                    S.add("dve", lambda e, sgb=sgb, pui=pui, ci=ci, b=b: e.tensor_tensor(
                        out=actT[:, ci, b * 512:(b + 1) * 512], in0=sg[sgb], in1=self.bank(pui), op=ALU.mult),
                        reads=[("sg", sgb), ("ps", pui)], writes=[("actT", ci, b)])
            for t in range(NT):
                for hf in range(2):
                    pdi = 4 + self.nxt("ffn_pd", 2)
                    for ci in range(C):
                        S.add("pe", lambda e, ci=ci, t=t, hf=hf, pdi=pdi, gb=gb, C=C: e.matmul(
                            self.bank(pdi), lhsT=actT[:, ci, t * 128:(t + 1) * 128],
                            rhs=wd_bf[gb][:, ci, hf * 512:(hf + 1) * 512], start=(ci == 0), stop=(ci == C - 1)),
                            reads=[("actT", ci, t // 4), ("wd_bf", gb, ci)], writes=[("ps", pdi)])
                    S.add("dve", lambda e, t=t, hf=hf, pdi=pdi: e.scalar_tensor_tensor(
                        out=self.X[:, t, hf * 512:(hf + 1) * 512], in0=self.bank(pdi), scalar=0.5,
                        in1=self.X[:, t, hf * 512:(hf + 1) * 512], op0=ALU.mult, op1=ALU.add),
                        reads=[("ps", pdi), ("X", t)], writes=[("X", t)])

    def load_w(self, dst_bf, src, ncols, stg, key, c0=0):
        S = self.S
        src_v = src.rearrange("(kc p) n -> p kc n", p=128)
        for kc in range(8):
            sb = self.nxt(key, 2)
            S.add("sp", lambda e, sb=sb, kc=kc: e.dma_start(out=stg[sb][:, 0:ncols], in_=src_v[:, kc, c0:c0 + ncols]),
                  writes=[(key, sb)], dma_key="%s%d" % (key, sb))
            S.add("pool", lambda e, sb=sb, kc=kc: e.tensor_copy(out=dst_bf[:, kc, :], in_=stg[sb][:, 0:ncols]),
                  reads=[(key, sb)], writes=[(id(dst_bf), kc)])
        return [(id(dst_bf), kc) for kc in range(8)]

    def restore_X(self):
        S = self.S
        S.barrier()
        for t in range(NT):
            S.add("sp", lambda e, t=t: e.dma_start(out=self.X[:, t, :], in_=self.xs_dram[:, t * D:(t + 1) * D]),
                  reads=[("xs", t // 4)], writes=[("X", t)], dma_key="xr%d" % t)
        S.barrier()

    def spill_X(self):
        S = self.S
        for q in range(4):
            S.add("sp", lambda e, q=q: e.dma_start(
                out=self.xs_dram[:, q * 4 * D:(q + 1) * 4 * D],
                in_=self.X[:, 4 * q:4 * q + 4, :].rearrange("p t d -> p (t d)")),
                reads=[("X", 4 * q + i) for i in range(4)], writes=[("xs", q)], dma_key="xs%d" % q)

    def out_proj(self, w_out_bf, wkeys):
        S = self.S
        for t in range(NT):
            S.add("sp", lambda e, t=t: e.dma_start(out=self.X[:, t, :], in_=self.xs_dram[:, t * D:(t + 1) * D]),
                  reads=[("xs", t // 4)], writes=[("X", t)], dma_key="xr%d" % t)
            for hf in range(2):
                pb = self.nxt("op_ps", 2)
                for kc in range(8):
                    S.add("pe", lambda e, kc=kc, t=t, hf=hf, pb=pb: e.matmul(
                        self.bank(pb), lhsT=self.HT[:, kc, t * 128:(t + 1) * 128],
                        rhs=w_out_bf[:, kc, hf * 512:(hf + 1) * 512], start=(kc == 0), stop=(kc == 7)),
                        reads=[("HT", t), wkeys[kc]], writes=[("ps", pb)])
                S.add("dve", lambda e, t=t, hf=hf, pb=pb: e.tensor_tensor(
                    out=self.X[:, t, hf * 512:(hf + 1) * 512], in0=self.bank(pb),
                    in1=self.X[:, t, hf * 512:(hf + 1) * 512], op=ALU.add),
                    reads=[("ps", pb), ("X", t)], writes=[("X", t)])

    def finish_heads(self, ps_o, ncols, oev, rcs, ps_b_idx, writes_fn, extra_den=None, o_reads=()):
        S = self.S
        ob = self.nxt("oev", 2)
        S.add("act", lambda e: e.activation(out=oev[ob][0:65, 0:ncols], in_=ps_o[0:65, 0:ncols], func=AF.Copy),
              reads=list(o_reads), writes=[("oev", ob)])
        rb = self.nxt("rcs", 2)
        if extra_den is not None:
            S.add("dve", lambda e: e.tensor_tensor(out=rcs[rb][64:65, 0:ncols], in0=oev[ob][64:65, 0:ncols],
                                                   in1=extra_den, op=ALU.add),
                  reads=[("oev", ob), "sinkexp"], writes=[("rcs", rb)])
            S.add("dve", lambda e: e.reciprocal(out=rcs[rb][64:65, 0:ncols], in_=rcs[rb][64:65, 0:ncols]),
                  reads=[("rcs", rb)], writes=[("rcs", rb)])
        else:
            S.add("dve", lambda e: e.reciprocal(out=rcs[rb][64:65, 0:ncols], in_=oev[ob][64:65, 0:ncols]),
                  reads=[("oev", ob)], writes=[("rcs", rb)])
        pbk = self.bank(ps_b_idx)
        S.add("pe", lambda e: e.matmul(pbk[0:64, 0:ncols], lhsT=self.ones_f[64:65, 0:64], rhs=rcs[rb][64:65, 0:ncols],
                                       start=True, stop=True),
              reads=[("rcs", rb), "ones_f"], writes=[("ps", ps_b_idx)])
        writes_fn(oev[ob], pbk, [("oev", ob), ("ps", ps_b_idx)])

    def mixer_even(self):
        S = self.S
        ar = self.ar
        self.norm_to_HT(1)
        self.spill_X()
        S.barrier()
        if self.stop_after == "L0_mix:a0":
            return self.restore_X()
        ar.reset(self.x_base)
        QT = ar.alloc("QT", [128, 8, T], BF16)
        KbT = ar.alloc("KbT", [128, 18 * 128], BF16)
        Vb = ar.alloc("Vb", [128, 18, 2, 65], BF16)
        hB = [ar.alloc("hB", [128, 258], BF16) for _ in range(2)]

        def kb_tile(kt, gs):
            if kt == 0:
                return hB[0][gs, 0:128]
            if kt == 17:
                return hB[1][gs, 0:128]
            return KbT[gs, kt * 128:(kt + 1) * 128]

        def vb_tile(kt, g):
            if kt == 0:
                return hB[0][:, 128:258].rearrange("p (k d) -> p k d", k=2)[:, g, :]
            if kt == 17:
                return hB[1][:, 128:258].rearrange("p (k d) -> p k d", k=2)[:, g, :]
            return Vb[:, kt, g, :]
        oev = [ar.alloc("oev", [128, 512], F32) for _ in range(2)]
        rcs = [ar.alloc("rcs", [128, 512], F32) for _ in range(2)]
        sinkexp = ar.alloc("sinkexp", [128, 1024], F32)
        bedge = ar.alloc("bedge", [128, 2], F32)
        maskB = [ar.alloc("maskB", [128, 512], BF16) for _ in range(2)]
        maskf = ar.alloc("maskf", [128, 128], F32)
        top = Arena(self.nc, self.total - 24576 - 128, self.total)
        ar.limit = top.base
        w_out_bf = top.alloc("w_out_bf", [128, 8, D], BF16)
        stg_o = [top.alloc("stg_o", [128, D], F32) for _ in range(2)]
        regA = ar.mark()
        KaT = ar.alloc("KaT", [128, T], BF16)
        Va = ar.alloc("Va", [128, NT, 2, 65], BF16)
        w_in_bf = ar.alloc("w_in_bf", [128, 8, 1536], BF16)
        stg_w = [ar.alloc("stg_w", [128, 1536], F32) for _ in range(2)]
        pj = [ar.alloc("pj", [128, 1536], F32) for _ in range(2)]
        ropet = [ar.alloc("ropet", [128, 256], F32) for _ in range(2)]
        GA = ar.alloc("GA", [128, 640], F32)
        sq = ar.alloc("sq", [128, 640], F32)
        ssq = ar.alloc("ssq", [128, 10], F32)
        qn = ar.alloc("qn", [128, 640], F32)
        t1 = ar.alloc("t1", [128, 640], F32)
        t2 = ar.alloc("t2", [128, 640], F32)
        t1b = ar.alloc("t1b", [128, 640], F32)
        t2b = ar.alloc("t2b", [128, 640], F32)
        rbuf = [ar.alloc("rbuf", [128, 1280], BF16) for _ in range(2)]

        S.add("sp", lambda e: e.dma_start(out=GA, in_=self.ga_in[0:1, :].broadcast_to([128, 640])),
              writes=["GA"], dma_key="GA")
        S.add("sp", lambda e: e.dma_start(out=bedge, in_=self.bedge_in), writes=["bedge"], dma_key="bedge")
        S.add("sp", lambda e: e.dma_start(out=sinkexp[64:65, :], in_=self.sink_in[0:1, :]), writes=["sinkexp"],
              dma_key="sinkexp")
        S.add("act", lambda e: e.activation(out=sinkexp[64:65, :], in_=sinkexp[64:65, :], func=AF.Exp),
              reads=["sinkexp"], writes=["sinkexp"])
        S.add("pool", lambda e: e.memset(Vb, 1.0), writes=["Vb_init"])
        S.add("pool", lambda e: e.memset(Va, 1.0), writes=["Va_init"])
        for mi, (cm, st) in enumerate(((1, -1), (-1, 1))):
            S.add("pool", lambda e: e.memset(maskf, 0.0), writes=["maskf"])
            S.add("pool", lambda e, cm=cm, st=st: e.affine_select(out=maskf, in_=maskf, pattern=[[st, 128]],
                                                                  compare_op=ALU.is_ge, fill=NEG, base=0,
                                                                  channel_multiplier=cm),
                  reads=["maskf"], writes=["maskf"])
            S.add("pool", lambda e, mi=mi: e.tensor_copy(
                out=maskB[mi].rearrange("p (h n) -> p h n", h=4),
                in_=maskf.unsqueeze(1).broadcast_to([128, 4, 128])), reads=["maskf"], writes=[("maskB", mi)])
        wkeys_in = self.load_w(w_in_bf, self.w_in, 1536, stg_w, "stg_w")

        if self.stop_after == "L0_mix:a1":
            return self.restore_X()
        rope_v = self.rope_in.rearrange("(t p) c -> p t c", p=128)
        lvl = 99
        if self.stop_after and self.stop_after.startswith("L0_mix:a2."):
            lvl = int(self.stop_after.split(".")[1])
        for t in range(NT):
            rtb = self.nxt("ropet", 2)
            S.add("sp", lambda e, t=t, rtb=rtb: e.dma_start(out=ropet[rtb], in_=rope_v[:, t, :]),
                  writes=[("ropet", rtb)], dma_key="ropet%d" % rtb)
            pjb = self.nxt("pj", 2)
            pb0 = 3 * pjb
            for cb in range(3):
                for kc in range(8):
                    S.add("pe", lambda e, kc=kc, cb=cb, t=t, pb0=pb0: e.matmul(
                        self.bank(pb0 + cb), lhsT=self.HT[:, kc, t * 128:(t + 1) * 128],
                        rhs=w_in_bf[:, kc, cb * 512:(cb + 1) * 512], start=(kc == 0), stop=(kc == 7)),
                        reads=[("HT", t), wkeys_in[kc]], writes=[("ps", pb0 + cb)])
            S.add("act", lambda e, pjb=pjb, pb0=pb0: e.activation(out=pj[pjb], in_=self.bank(pb0, 3), func=AF.Copy),
                  reads=[("ps", pb0 + i) for i in range(3)], writes=[("pj", pjb)])
            P = pj[pjb]
            pa = P[:, 0:640]
            if lvl < 2:
                continue
            S.add("dve", lambda e, pa=pa: e.tensor_tensor(out=sq, in0=pa, in1=pa, op=ALU.mult),
                  reads=[("pj", pjb)], writes=["sq"])
            S.add("dve", lambda e: e.reduce_sum(out=ssq, in_=sq.rearrange("p (h d) -> p h d", d=64), axis=AX.X),
                  reads=["sq"], writes=["ssq"])
            S.add("act", lambda e: e.activation(out=ssq, in_=ssq, func=AF.Sqrt, scale=1.0 / 64, bias=EPS),
                  reads=["ssq"], writes=["ssq"])
            S.add("dve", lambda e: e.reciprocal(out=ssq, in_=ssq), reads=["ssq"], writes=["ssq"])
            S.add("dve", lambda e, pa=pa: e.tensor_tensor(
                out=qn.rearrange("p (h d) -> p h d", d=64), in0=pa.rearrange("p (h d) -> p h d", d=64),
                in1=ssq.unsqueeze(2).broadcast_to([128, 10, 64]), op=ALU.mult),
                reads=[("pj", pjb), "ssq"], writes=["qn"])
            S.add("pool", lambda e: e.tensor_tensor(out=qn, in0=qn, in1=GA, op=ALU.mult), reads=["qn", "GA"],
                  writes=["qn"])
            rt = ropet[rtb]
            if lvl < 3:
                continue
            S.add("dve", lambda e, rt=rt: e.tensor_tensor(
                out=t1.rearrange("p (h d) -> p h d", d=64), in0=qn.rearrange("p (h d) -> p h d", d=64),
                in1=rt[:, 0:64].unsqueeze(1).broadcast_to([128, 10, 64]), op=ALU.mult),
                reads=["qn", ("ropet", rtb)], writes=["t1"])
            for hf in range(2):
                S.add("pool", lambda e, rt=rt, hf=hf: e.tensor_tensor(
                    out=t2.rearrange("p (h b f d) -> p h b f d", b=2, f=2, d=16)[:, :, :, hf, :],
                    in0=qn.rearrange("p (h b f d) -> p h b f d", b=2, f=2, d=16)[:, :, :, 1 - hf, :],
                    in1=rt[:, 64:128].rearrange("p (b f d) -> p b f d", b=2, f=2)[:, :, hf, :].unsqueeze(1).broadcast_to(
                        [128, 10, 2, 16]), op=ALU.mult), reads=["qn", ("ropet", rtb)], writes=[("t2", hf)])
            rbb = self.nxt("rbuf", 2)
            RB = rbuf[rbb]
            S.add("dve", lambda e, RB=RB: e.tensor_tensor(
                out=RB[:, 0:512].rearrange("p (j hi d) -> p hi j d", hi=2, d=64),
                in0=t1[:, 0:512].rearrange("p (hi j d) -> p hi j d", hi=2, d=64),
                in1=t2[:, 0:512].rearrange("p (hi j d) -> p hi j d", hi=2, d=64), op=ALU.add),
                reads=["t1", ("t2", 0), ("t2", 1)], writes=[("rbuf", rbb, 0)])
            S.add("dve", lambda e, RB=RB: e.tensor_tensor(out=RB[:, 1024:1152], in0=t1[:, 512:640], in1=t2[:, 512:640],
                                                          op=ALU.add),
                  reads=["t1", ("t2", 0), ("t2", 1)], writes=[("rbuf", rbb, 1)])
            if lvl < 4:
                continue
            pbv = P[:, 768:1408]
            S.add("dve", lambda e, rt=rt, pbv=pbv: e.tensor_tensor(
                out=t1b.rearrange("p (h d) -> p h d", d=64), in0=pbv.rearrange("p (h d) -> p h d", d=64),
                in1=rt[:, 128:192].unsqueeze(1).broadcast_to([128, 10, 64]), op=ALU.mult),
                reads=[("pj", pjb), ("ropet", rtb)], writes=["t1b"])
            for hf in range(2):
                S.add("pool", lambda e, rt=rt, hf=hf, pbv=pbv: e.tensor_tensor(
                    out=t2b.rearrange("p (h f d) -> p h f d", f=2, d=32)[:, :, hf, :],
                    in0=pbv.rearrange("p (h f d) -> p h f d", f=2, d=32)[:, :, 1 - hf, :],
                    in1=rt[:, 192:256].rearrange("p (f d) -> p f d", f=2)[:, hf, :].unsqueeze(1).broadcast_to(
                        [128, 10, 32]), op=ALU.mult), reads=[("pj", pjb), ("ropet", rtb)], writes=[("t2b", hf)])
            S.add("dve", lambda e, RB=RB: e.tensor_tensor(
                out=RB[:, 512:1024].rearrange("p (j hi d) -> p hi j d", hi=2, d=64),
                in0=t1b[:, 0:512].rearrange("p (hi j d) -> p hi j d", hi=2, d=64),
                in1=t2b[:, 0:512].rearrange("p (hi j d) -> p hi j d", hi=2, d=64), op=ALU.add),
                reads=["t1b", ("t2b", 0), ("t2b", 1)], writes=[("rbuf", rbb, 2)])
            S.add("dve", lambda e, RB=RB: e.tensor_tensor(out=RB[:, 1152:1280], in0=t1b[:, 512:640],
                                                          in1=t2b[:, 512:640], op=ALU.add),
                  reads=["t1b", ("t2b", 0), ("t2b", 1)], writes=[("rbuf", rbb, 3)])
            if lvl < 5:
                continue
            S.add("act", lambda e, P=P, t=t: e.activation(out=Va[:, t, :, 0:64],
                                                         in_=P[:, 640:768].rearrange("p (k d) -> p k d", d=64),
                                                         func=AF.Copy),
                  reads=[("pj", pjb), "Va_init"], writes=[("Va", t)])
            S.add("act", lambda e, P=P, t=t: e.activation(out=Vb[:, t + 1, :, 0:64],
                                                         in_=P[:, 1408:1536].rearrange("p (k d) -> p k d", d=64),
                                                         func=AF.Copy),
                  reads=[("pj", pjb), "Vb_init"], writes=[("Vb", t + 1)])
            if lvl < 6:
                continue
            tb = 6
            for blk in range(10):
                S.add("pe", lambda e, blk=blk, RB=RB: e.transpose(
                    out=self.bankb(tb, 2)[:, blk * 128:(blk + 1) * 128], in_=RB[:, blk * 128:(blk + 1) * 128],
                    identity=self.ident), reads=[("rbuf", rbb, i) for i in range(4)] + ["ident"],
                    writes=[("ps", 6), ("ps", 7)])
            pst = self.bankb(tb, 2).rearrange("p (k n) -> p k n", n=128)
            tsl = slice(t * 128, (t + 1) * 128)
            S.add("act", lambda e, pst=pst, tsl=tsl: e.activation(out=QT[:, 0:4, tsl], in_=pst[:, 0:4, :], func=AF.Copy),
                  reads=[("ps", 6), ("ps", 7)], writes=[("QTa", t)])
            S.add("act", lambda e, pst=pst, tsl=tsl: e.activation(out=KaT[:, tsl], in_=pst[:, 8, :], func=AF.Copy),
                  reads=[("ps", 6), ("ps", 7)], writes=[("KaT", t)])
            S.add("act", lambda e, pst=pst, tsl=tsl: e.activation(out=QT[:, 4:8, tsl], in_=pst[:, 4:8, :], func=AF.Copy),
                  reads=[("ps", 6), ("ps", 7)], writes=[("QTb", t)])
            S.add("act", lambda e, pst=pst, t=t: e.activation(out=KbT[:, (t + 1) * 128:(t + 2) * 128], in_=pst[:, 9, :], func=AF.Copy),
                  reads=[("ps", 6), ("ps", 7)], writes=[("KbT", t + 1)])

        if self.stop_after.startswith("L0_mix:a2") if self.stop_after else False:
            return self.restore_X()
        mA = self.mineA.ap()
        st_ops = []
        S.add("sp", lambda e: e.dma_start(out=mA[:, 0:2048], in_=KaT), reads=[("KaT", t) for t in range(NT)],
              writes=["mineA0"], dma_key="mA0")
        S.add("sp", lambda e: e.dma_start(out=mA[:, 2048:4128], in_=Va.rearrange("p t k d -> p (t k d)")),
              reads=[("Va", t) for t in range(NT)], writes=["mineA1"], dma_key="mA1")
        S.add("sp", lambda e: e.dma_start(out=mA[:, 4128:4256], in_=KbT[:, 128:256]), reads=[("KbT", 1)],
              writes=["mineA2"], dma_key="mA2")
        S.add("sp", lambda e: e.dma_start(out=mA[:, 4256:4386], in_=Vb[:, 1, :, :].rearrange("p k d -> p (k d)")),
              reads=[("Vb", 1)], writes=["mineA3"], dma_key="mA3")
        S.add("sp", lambda e: e.dma_start(out=mA[:, 4386:4514], in_=KbT[:, 16 * 128:17 * 128]), reads=[("KbT", 16)],
              writes=["mineA4"], dma_key="mA4")
        S.add("sp", lambda e: e.dma_start(out=mA[:, 4514:4644], in_=Vb[:, 16, :, :].rearrange("p k d -> p (k d)")),
              reads=[("Vb", 16)], writes=["mineA5"], dma_key="mA5")
        S.barrier()
        if self.stop_after == "L0_mix:a":
            return self.restore_X()
        S.add("pool", lambda e: e.collective_compute("AllGather", ALU.bypass, replica_groups=[list(range(NCORES))],
                                                     ins=[self.mineA.ap().opt()], outs=[self.gathA.ap().opt()]),
              reads=["mineA%d" % i for i in range(6)], writes=["gathA_"])
        S.add("pool", lambda e: e.collective_compute("AllGather", ALU.bypass, replica_groups=[list(range(NCORES))],
                                                     ins=[self.dmy_in.ap().opt()], outs=[self.dmy_out[0].ap().opt()]),
              reads=["gathA_"], writes=["gathA"])
        ar.reset(regA)
        Kall = ar.alloc("Kall", [128, NCORES * T], BF16)
        Vall = ar.alloc("Vall", [128, NCORES * NT, 2, 65], BF16)
        pTa = [ar.alloc("pTa", [128, 1024], BF16) for _ in range(3)]
        pTb = [ar.alloc("pTb", [128, 1536], BF16) for _ in range(2)]
        gA = self.gathA.ap()
        wkeys_out = self.load_w(w_out_bf, self.w_out0, D, stg_o, "stg_o")
        S.add("sp", lambda e: e.dma_start(out=hB[0], in_=gA[bass.ds(self.rrow(e, 0), 128), 4386:4644]),
              reads=["gathA"], writes=[("KbT", 0), ("Vb", 0)], dma_key="hb0")
        S.add("sp", lambda e: e.dma_start(out=hB[1], in_=gA[bass.ds(self.rrow(e, 1), 128), 4128:4386]),
              reads=["gathA"], writes=[("KbT", 17), ("Vb", 17)], dma_key="hb1")
        for r in range(NCORES):
            S.add("sp", lambda e, r=r: e.dma_start(out=Kall[:, r * T:(r + 1) * T], in_=gA[r * 128:(r + 1) * 128, 0:2048]),
                  reads=["gathA"], writes=[("Kall", r)], dma_key="Kall%d" % r)
            S.add("sp", lambda e, r=r: e.dma_start(
                out=Vall[:, r * NT:(r + 1) * NT, :, :].rearrange("p t k d -> p (t k d)"),
                in_=gA[r * 128:(r + 1) * 128, 2048:4128]), reads=["gathA"], writes=[("Vall", r)], dma_key="Vall%d" % r)

        if self.stop_after == "L0_mix:b":
            return self.restore_X()
        for t in range(NT):
            for g in range(2):
                sbi = self.nxt("B_s", 2)
                pb0 = 3 * sbi
                gs = slice(g * 64, (g + 1) * 64)
                for j in range(3):
                    kt = t + j
                    S.add("pe", lambda e, j=j, kt=kt, gs=gs, t=t, pb0=pb0: e.matmul(
                        self.bank(pb0 + j), lhsT=kb_tile(kt, gs),
                        rhs=QT[gs, 4:8, t * 128:(t + 1) * 128], start=True, stop=(j == 1)),
                        reads=[("KbT", kt), ("QTb", t)], writes=[("ps", pb0 + j)])
                    if j != 1:
                        S.add("pe", lambda e, j=j, pb0=pb0: e.matmul(
                            self.bank(pb0 + j), lhsT=self.ident, rhs=maskB[j // 2], start=False, stop=True),
                            reads=["ident", ("maskB", j // 2)], writes=[("ps", pb0 + j)])
                PT = pTb[sbi]
                if t == 0 or t == NT - 1:
                    for j in range(3):
                        bias = None
                        if t == 0 and j == 0:
                            bias = bedge[:, 0:1]
                        if t == NT - 1 and j == 2:
                            bias = bedge[:, 1:2]
                        if bias is None:
                            S.add("act", lambda e, j=j, PT=PT, pb0=pb0: e.activation(
                                out=PT[:, j * 512:(j + 1) * 512], in_=self.bank(pb0 + j), func=AF.Exp, scale=SCALE),
                                reads=[("ps", pb0 + j)], writes=[("pTb", sbi, j)])
                        else:
                            S.add("act", lambda e, j=j, PT=PT, pb0=pb0, bias=bias: e.activation(
                                out=PT[:, j * 512:(j + 1) * 512], in_=self.bank(pb0 + j), func=AF.Exp, scale=SCALE,
                                bias=bias), reads=[("ps", pb0 + j), "bedge"], writes=[("pTb", sbi, j)])
                else:
                    S.add("act", lambda e, PT=PT, pb0=pb0: e.activation(out=PT, in_=self.bank(pb0, 3), func=AF.Exp,
                                                                        scale=SCALE),
                          reads=[("ps", pb0 + j) for j in range(3)], writes=[("pTb", sbi, j) for j in range(3)])
                for hh in range(4):
                    for j in range(3):
                        S.add("pe", lambda e, hh=hh, j=j, t=t, g=g, PT=PT: e.matmul(
                            self.bank(6)[0:65, hh * 128:(hh + 1) * 128], lhsT=vb_tile(t + j, g),
                            rhs=PT[:, j * 512 + hh * 128: j * 512 + (hh + 1) * 128], start=(j == 0), stop=(j == 2)),
                            reads=[("Vb", t + j), ("pTb", sbi, j)], writes=[("ps", 6)])

                def wr(ov, pbk, rds, t=t, g=g):
                    for half in range(2):
                        S.add("dve", lambda e, half=half: e.tensor_tensor(
                            out=self.HT[half * 64:(half + 1) * 64, 4 + 2 * g:6 + 2 * g, t * 128:(t + 1) * 128],
                            in0=ov[0:64, :].rearrange("p (pp hf n) -> p hf pp n", pp=2, hf=2)[:, half, :, :],
                            in1=pbk[0:64, :].rearrange("p (pp hf n) -> p hf pp n", pp=2, hf=2)[:, half, :, :],
                            op=ALU.mult), reads=rds, writes=[("HT", t)])
                self.finish_heads(self.bank(6), 512, oev, rcs, 7, wr,
                                  extra_den=sinkexp[64:65, g * 512:(g + 1) * 512], o_reads=[("ps", 6)])

        if self.stop_after == "L0_mix:c":
            return self.restore_X()
        NK = NCORES * NT
        for qb in range(4):
            qs = slice(qb * 512, (qb + 1) * 512)
            for j in range(4):
                def qk(kt, j=j, qs=qs):
                    sbi = kt % 2
                    for hi in range(2):
                        hs = slice(hi * 64, (hi + 1) * 64)
                        S.add("pe", lambda e, hs=hs, hi=hi, kt=kt, sbi=sbi: e.matmul(
                            self.bank(2 * sbi + hi), lhsT=Kall[hs, kt * 128:(kt + 1) * 128], rhs=QT[hs, j, qs],
                            start=True, stop=True),
                            reads=[("Kall", kt // NT)] + [("QTa", 4 * qb + i) for i in range(4)],
                            writes=[("ps", 2 * sbi + hi)])

                def ex(kt):
                    sbi = kt % 2
                    pi = kt % 3
                    S.add("act", lambda e, sbi=sbi, pi=pi: e.activation(out=pTa[pi], in_=self.bank(2 * sbi, 2),
                                                                        func=AF.Exp, scale=SCALE),
                          reads=[("ps", 2 * sbi), ("ps", 2 * sbi + 1)], writes=[("pTa", pi)])

                def pv(kt):
                    pi = kt % 3
                    for hi in range(2):
                        S.add("pe", lambda e, hi=hi, kt=kt, pi=pi: e.matmul(
                            self.bank(4 + hi)[0:65, :], lhsT=Vall[:, kt, hi, :], rhs=pTa[pi][:, hi * 512:(hi + 1) * 512],
                            start=(kt == 0), stop=(kt == NK - 1)),
                            reads=[("Vall", kt // NT), ("pTa", pi)], writes=[("ps", 4 + hi)])
                qk(0)
                for kt in range(NK):
                    if kt + 1 < NK:
                        qk(kt + 1)
                    ex(kt)
                    pv(kt)
                for hi in range(2):
                    h = j + 4 * hi
                    kc, half = h // 2, h % 2

                    def wr(ov, pbk, rds, kc=kc, half=half, qs=qs, qb=qb):
                        S.add("dve", lambda e: e.tensor_tensor(
                            out=self.HT[half * 64:(half + 1) * 64, kc, qs], in0=ov[0:64, :], in1=pbk[0:64, :],
                            op=ALU.mult), reads=rds, writes=[("HT", 4 * qb + i) for i in range(4)])
                    self.finish_heads(self.bank(4 + hi), 512, oev, rcs, 7, wr, o_reads=[("ps", 4 + hi)])

        if self.stop_after == "L0_mix:d":
            return self.restore_X()
        S.barrier()
        self.out_proj(w_out_bf, wkeys_out)
        S.barrier()

    def mixer_odd(self):
        S = self.S
        ar = self.ar
        self.norm_to_HT(4)
        self.spill_X()
        S.barrier()
        ar.reset(self.x_base)
        top = Arena(self.nc, self.total - 24576 - 128, self.total)
        ar.limit = top.base
        w_out_bf = top.alloc("w_out_bf1", [128, 8, D], BF16)
        stg_o = [top.alloc("stg_o1", [128, D], F32) for _ in range(2)]
        oev = [ar.alloc("oev1", [128, 512], F32) for _ in range(2)]
        rcs = [ar.alloc("rcs1", [128, 512], F32) for _ in range(2)]
        rowb = ar.alloc("rowb", [128, 16 * 7 * 2], F32)
        regA = ar.mark()
        wpart = [ar.alloc("wpart", [128, 8, D], BF16) for _ in range(2)]
        stg_q = [ar.alloc("stg_q", [128, D], F32) for _ in range(2)]
        qst = [ar.alloc("qst", [128, 512], BF16) for _ in range(2)]
        vst = [ar.alloc("vst", [128, 8, 65], BF16) for _ in range(2)]
        S.add("sp", lambda e: e.dma_start(out=rowb, in_=self.rowbias_in), writes=["rowb"], dma_key="rowb")
        for i in range(2):
            S.add("pool", lambda e, i=i: e.memset(vst[i], 1.0), writes=[("vst", i)])
        for part, dst in ((0, self.qc_dram), (1, self.kc_dram)):
            wpi = part % 2
            wk = self.load_w(wpart[wpi], self.w_qkv, D, stg_q, "stg_q", c0=part * D)
            for jp in range(8):
                for b in range(4):
                    pb = self.nxt("qk_ps", 4)
                    for kc in range(8):
                        S.add("pe", lambda e, kc=kc, jp=jp, b=b, pb=pb, wpi=wpi: e.matmul(
                            self.bank(pb), lhsT=wpart[wpi][:, kc, jp * 128:(jp + 1) * 128],
                            rhs=self.HT[:, kc, b * 512:(b + 1) * 512], start=(kc == 0), stop=(kc == 7)),
                            reads=[wk[kc]] + [("HT", 4 * b + i) for i in range(4)], writes=[("ps", pb)])
                    qb_ = self.nxt("qst", 2)
                    eng = "act"
                    if eng == "act":
                        S.add("act", lambda e, pb=pb, qb_=qb_: e.activation(out=qst[qb_], in_=self.bank(pb), func=AF.Copy),
                              reads=[("ps", pb)], writes=[("qst", qb_)])
                    else:
                        S.add("dve", lambda e, pb=pb, qb_=qb_: e.tensor_copy(out=qst[qb_], in_=self.bank(pb)),
                              reads=[("ps", pb)], writes=[("qst", qb_)])
                    S.add("sp", lambda e, qb_=qb_, jp=jp, b=b, dst=dst: e.dma_start(
                        out=dst[jp, :, b * 512:(b + 1) * 512], in_=qst[qb_]),
                        reads=[("qst", qb_)], writes=[("qkd", part, jp)], dma_key="qst%d" % qb_)
        wk = self.load_w(wpart[0], self.w_qkv, D, stg_q, "stg_q", c0=2 * D)
        for t in range(NT):
            for cb in range(2):
                pb = self.nxt("qk_ps", 4)
                for kc in range(8):
                    S.add("pe", lambda e, kc=kc, t=t, cb=cb, pb=pb: e.matmul(
                        self.bank(pb), lhsT=self.HT[:, kc, t * 128:(t + 1) * 128],
                        rhs=wpart[0][:, kc, cb * 512:(cb + 1) * 512], start=(kc == 0), stop=(kc == 7)),
                        reads=[wk[kc], ("HT", t)], writes=[("ps", pb)])
                vb_ = self.nxt("vst", 2)
                S.add("act", lambda e, pb=pb, vb_=vb_: e.activation(
                    out=vst[vb_][:, :, 0:64], in_=self.bank(pb).rearrange("p (h d) -> p h d", d=64), func=AF.Copy),
                    reads=[("ps", pb)], writes=[("vst", vb_)])
                S.add("sp", lambda e, vb_=vb_, t=t, cb=cb: e.dma_start(
                    out=self.vc_dram[:, t, cb * 520:(cb + 1) * 520], in_=vst[vb_].rearrange("p h d -> p (h d)")),
                    reads=[("vst", vb_)], writes=[("vd", t)], dma_key="vst%d" % vb_)
        S.barrier()
        if self.stop_after == "L1_mix:a":
            return self.restore_X()
        mC = self.mineC.ap()
        S.add("sp", lambda e: e.dma_start(out=mC[:, 64:2112].rearrange("p (j n) -> p j n", j=8),
                                          in_=self.kc_dram[:, :, 0:256].rearrange("j p n -> p j n")),
              writes=["mC0"], dma_key="mC0")
        S.add("sp", lambda e: e.dma_start(out=mC[:, 2112:4192], in_=self.vc_dram[:, 0:2, :].rearrange("p t c -> p (t c)")),
              writes=["mC1"], dma_key="mC1")
        S.add("sp", lambda e: e.dma_start(out=mC[:, 4192:6240].rearrange("p (j n) -> p j n", j=8),
                                          in_=self.kc_dram[:, :, T - 256:T].rearrange("j p n -> p j n")),
              writes=["mC2"], dma_key="mC2")
        S.add("sp", lambda e: e.dma_start(out=mC[:, 6240:8320],
                                          in_=self.vc_dram[:, NT - 2:NT, :].rearrange("p t c -> p (t c)")),
              writes=["mC3"], dma_key="mC3")
        S.barrier()
        S.add("pool", lambda e: e.collective_compute("AllGather", ALU.bypass, replica_groups=[list(range(NCORES))],
                                                     ins=[self.mineC.ap().opt()], outs=[self.gathC.ap().opt()]),
              reads=["mC%d" % i for i in range(4)], writes=["gathC_"])
        S.add("pool", lambda e: e.collective_compute("AllGather", ALU.bypass, replica_groups=[list(range(NCORES))],
                                                     ins=[self.dmy_in.ap().opt()], outs=[self.dmy_out[1].ap().opt()]),
              reads=["gathC_"], writes=["gathC"])
        wkeys_out = self.load_w(w_out_bf, self.w_out1, D, stg_o, "stg_o1")
        ar.reset(regA)
        NB = 2
        QTg = [ar.alloc("QTg", [128, 2, T], BF16) for _ in range(NB)]
        KTg = [ar.alloc("KTg", [128, 2, 20 * 128], BF16) for _ in range(NB)]
        VGg = [ar.alloc("VGg", [128, 20, 4, 65], BF16) for _ in range(NB)]
        hC = [ar.alloc("hC", [128, 4128], BF16) for _ in range(2)]
        MTg = [ar.alloc("MTg", [128, 7, 512], F32) for _ in range(NB)]
        tmp = [ar.alloc("tmp", [128, 512], F32) for _ in range(2)]
        pT = [ar.alloc("pTc", [128, 6, 512], BF16) for _ in range(2)]
        gC = self.gathC.ap()
        S.add("sp", lambda e: (setattr(self, "_rrow", None), e.nop())[1])
        S.add("sp", lambda e: e.dma_start(out=hC[0], in_=gC[bass.ds(self.rrow(e, 0), 128), 4192:8320]),
              reads=["gathC"], writes=["hC"], dma_key="hC0")
        S.add("sp", lambda e: e.dma_start(out=hC[1], in_=gC[bass.ds(self.rrow(e, 1), 128), 64:4192]),
              reads=["gathC"], writes=["hC"], dma_key="hC1")
        if self.stop_after == "L1_mix:b":
            return self.restore_X()
        for g4 in range(4 if self.stop_after != "L1_mix:c1" else 1):
            gb = g4 % NB
            Q, K, V, M = QTg[gb], KTg[gb], VGg[gb], MTg[gb]
            S.add("sp", lambda e, g4=g4, Q=Q: e.dma_start(out=Q, in_=self.qc_dram[2 * g4:2 * g4 + 2].rearrange("j p n -> p j n")),
                  writes=[("QTg", gb)], dma_key="QTg%d" % gb)
            S.add("sp", lambda e, g4=g4, K=K: e.dma_start(out=K[:, :, 256:256 + T],
                                                          in_=self.kc_dram[2 * g4:2 * g4 + 2].rearrange("j p n -> p j n")),
                  writes=[("KTg", gb, 1)], dma_key="KTg%d" % gb)
            S.add("sp", lambda e, g4=g4, V=V: e.dma_start(
                out=V[:, 2:18, :, :].rearrange("p t h d -> p t (h d)"), in_=self.vc_dram[:, :, g4 * 260:(g4 + 1) * 260]),
                writes=[("VGg", gb, 1)], dma_key="VGg%d" % gb)
            S.add("sp", lambda e, g4=g4, M=M: e.dma_start(out=M.rearrange("p u c -> p (u c)"), in_=self.mt_in[g4]),
                  writes=[("MTg", gb)], dma_key="MTg%d" % gb)
            for wh in range(2):
                ksl = slice(0, 256) if wh == 0 else slice(256 + T, 512 + T)
                vsl = slice(0, 2) if wh == 0 else slice(18, 20)
                S.add("pool", lambda e, g4=g4, K=K, wh=wh, ksl=ksl: e.tensor_copy(
                    out=K[:, :, ksl],
                    in_=hC[wh][:, 0:2048].rearrange("p (j n) -> p j n", j=8)[:, 2 * g4:2 * g4 + 2, :]),
                    reads=["hC"], writes=[("KTg", gb, 0 if wh == 0 else 2)])
                S.add("pool", lambda e, g4=g4, V=V, wh=wh, vsl=vsl: e.tensor_copy(
                    out=V[:, vsl, :, :],
                    in_=hC[wh][:, 2048:4128].rearrange("p (t h d) -> p t h d", t=2, h=16)[:, :, 4 * g4:4 * g4 + 4, :]),
                    reads=["hC"], writes=[("VGg", gb, 0 if wh == 0 else 2)])

            def k_tile(kt, hs, pp, K=K):
                return K[hs, pp, kt * 128:(kt + 1) * 128]

            def v_tile(kt, hh, V=V):
                return V[:, kt, hh, :]

            import os
            for rp in ([int(v) for v in os.environ['NA_RPS'].split(',')] if os.environ.get('NA_RPS') else range(16)):
                us = list(range(1, 7)) if rp == 0 else (list(range(0, 6)) if rp == 15 else list(range(1, 6)))
                obk = 4 + self.nxt("na_o", 2)
                pset = self.nxt("na_pset", 2)
                for ui, u in enumerate(us):
                    kt = rp + u - 1
                    seg = 0 if kt < 2 else (2 if kt >= 18 else 1)
                    sbk = 2 * self.nxt("na_s", 2)
                    for pp in range(2):
                        for half in range(2):
                            hs = slice(half * 64, (half + 1) * 64)
                            kap = k_tile(kt, hs, pp)
                            S.add("pe", lambda e, hs=hs, pp=pp, half=half, kt=kt, rp=rp, sbk=sbk, Q=Q, kap=kap: e.matmul(
                                self.bank(sbk + half)[:, pp * 128:(pp + 1) * 128], lhsT=kap,
                                rhs=Q[hs, pp, rp * 128:(rp + 1) * 128], start=True, stop=True),
                                reads=[("KTg", gb, seg), ("QTg", gb)], writes=[("ps", sbk + half)])
                    tb_ = self.nxt("na_tmp", 2)
                    for half in range(2):
                        S.add("dve", lambda e, sbk=sbk, tb_=tb_, u=u, M=M, half=half: e.scalar_tensor_tensor(
                            out=tmp[tb_].rearrange("p (pp hf n) -> p pp hf n", pp=2, hf=2)[:, :, half, :],
                            in0=self.bank(sbk + half)[:, 0:256].rearrange("p (pp n) -> p pp n", pp=2), scalar=SCALE,
                            in1=M[:, u, :].rearrange("p (pp hf n) -> p pp hf n", pp=2, hf=2)[:, :, half, :],
                            op0=ALU.mult, op1=ALU.add),
                            reads=[("ps", sbk + half), ("MTg", gb)], writes=[("tmp", tb_, half)])
                    for qp in range(2):
                        S.add("act", lambda e, tb_=tb_, pset=pset, ui=ui, qp=qp, rp=rp, u=u: e.activation(
                            out=pT[pset][:, ui, :].rearrange("p (h q n) -> p h q n", h=4, q=2)[:, :, qp, :],
                            in_=tmp[tb_].rearrange("p (h q n) -> p h q n", h=4, q=2)[:, :, qp, :], func=AF.Exp,
                            bias=rowb[:, (rp * 7 + u) * 2 + qp:(rp * 7 + u) * 2 + qp + 1]),
                            reads=[("tmp", tb_, 0), ("tmp", tb_, 1), "rowb"], writes=[("pTc", pset, ui, qp)])
                for hh in range(4):
                    for ui, u in enumerate(us):
                        kt = rp + u - 1
                        vap = v_tile(kt, hh)
                        S.add("pe", lambda e, hh=hh, ui=ui, n=len(us), obk=obk, vap=vap, pset=pset: e.matmul(
                            self.bank(obk)[0:65, hh * 128:(hh + 1) * 128], lhsT=vap,
                            rhs=pT[pset][:, ui, hh * 128:(hh + 1) * 128], start=(ui == 0), stop=(ui == n - 1)),
                            reads=[("VGg", gb, 0 if kt < 2 else (2 if kt >= 18 else 1)), ("pTc", pset, ui, 0),
                                   ("pTc", pset, ui, 1)], writes=[("ps", obk)])

                def wr(ov, pbk, rds, rp=rp, g4=g4):
                    for half in range(2):
                        S.add("dve", lambda e, half=half: e.tensor_tensor(
                            out=self.HT[half * 64:(half + 1) * 64, 2 * g4:2 * g4 + 2, rp * 128:(rp + 1) * 128],
                            in0=ov[0:64, :].rearrange("p (pp hf n) -> p hf pp n", pp=2, hf=2)[:, half, :, :],
                            in1=pbk[0:64, :].rearrange("p (pp hf n) -> p hf pp n", pp=2, hf=2)[:, half, :, :],
                            op=ALU.mult), reads=rds, writes=[("HT", rp)])
                self.finish_heads(self.bank(obk), 512, oev, rcs, 6 + self.nxt("na_b", 2), wr, o_reads=[("ps", obk)])
        if self.stop_after in ("L1_mix:c", "L1_mix:c1"):
            return self.restore_X()
        S.barrier()
        self.out_proj(w_out_bf, wkeys_out)
        S.barrier()

    def final(self, raw=False):
        S = self.S
        ar = self.ar
        ar.reset(self.free_base)
        ar.limit = self.total
        ys = [ar.alloc("ys", [128, D], F32) for _ in range(2)]
        out_v = self.out.rearrange("(t p) d -> p t d", p=128)
        last = []
        if raw:
            for t in range(NT):
                last.append(S.add("sp", lambda e, t=t: e.dma_start(out=out_v[:, t, :], in_=self.X[:, t, :]),
                                  reads=[("X", t)], dma_key="out%d" % (t % 4)))
            return last
        S.add("sp", lambda e: e.dma_start(out=self.Gb, in_=self.gains[6:7, :].broadcast_to([128, D])),
              writes=["Gb"], dma_key="Gb")
        self.rstd_all()
        for t in range(NT):
            yb = self.nxt("ys", 2)
            S.add("dve", lambda e, t=t, yb=yb: e.scalar_tensor_tensor(out=ys[yb], in0=self.X[:, t, :],
                                                                       scalar=self.rstd[:, t:t + 1], in1=self.Gb,
                                                                       op0=ALU.mult, op1=ALU.mult),
                  reads=[("X", t), ("rstd", t), "Gb"], writes=[("ys", yb)])
            last.append(S.add("sp", lambda e, t=t, yb=yb: e.dma_start(out=out_v[:, t, :], in_=ys[yb]),
                              reads=[("ys", yb)], dma_key="out%d" % yb))
        return last

    def build(self):
        S = self.S
        sa = self.stop_after
        self.consts()
        xv = self.x_in.rearrange("(t p) d -> p t d", p=128)
        for q in range(4):
            S.add("sp", lambda e, q=q: e.dma_start(out=self.X[:, 4 * q:4 * q + 4, :], in_=xv[:, 4 * q:4 * q + 4, :]),
                  writes=[("X", 4 * q + i) for i in range(4)], dma_key="xin%d" % q)
        stages = [
            ("L0_ffn1", lambda: (self.norm_to_HT(0), self.ffn(self.wg[0][0], self.wu[0][0], self.wd[0][0]))),
            ("L0_mix", self.mixer_even),
            ("L0_ffn2", lambda: (self.norm_to_HT(2), self.ffn(self.wg[1][0], self.wu[1][0], self.wd[1][0]))),
            ("L1_ffn1", lambda: (self.norm_to_HT(3), self.ffn(self.wg[0][1], self.wu[0][1], self.wd[0][1]))),
            ("L1_mix", self.mixer_odd),
            ("L1_ffn2", lambda: (self.norm_to_HT(5), self.ffn(self.wg[1][1], self.wu[1][1], self.wd[1][1]))),
        ]
        raw = False
        for name, fn in stages:
            if "ffn" in self.skip and "ffn" in name:
                continue
            if name in self.skip:
                continue
            fn()
            if sa is not None and sa.split(":")[0] == name:
                raw = True
                break
        last = self.final(raw=raw)
        S.emit(final_wait_ops=last)
        return self.nc


def _rope_tables():
    def cs(pos, dim):
        inv = np.float32(10000.0) ** (-np.arange(0, dim, 2, dtype=np.float32) / np.float32(dim))
        ang = pos.astype(np.float32)[:, None] * inv.astype(np.float32)[None, :]
        return np.cos(ang).astype(np.float32), np.sin(ang).astype(np.float32)
    pos = np.arange(NCORES * T)
    c1, s1 = cs(pos, 64)
    cr, sr = cs(pos // 64, 32)
    cc, sc = cs(pos % 64, 32)
    CA = np.concatenate([cr, cr, cc, cc], axis=1)
    SA = np.concatenate([-sr, sr, -sc, sc], axis=1)
    CB = np.concatenate([c1, c1], axis=1)
    SB = np.concatenate([-s1, s1], axis=1)
    return np.ascontiguousarray(np.concatenate([CA, SA, CB, SB], axis=1).astype(np.float32))


def _mt_table(rel_bias):
    kp = (np.arange(128) // 64)[:, None, None, None]
    kc = (np.arange(128) % 64)[:, None, None, None]
    u = np.arange(7)[None, :, None, None]
    qp = np.arange(2)[None, None, :, None]
    qc = np.arange(64)[None, None, None, :]
    dr = (-6 + 2 * u + kp) - qp
    dc = kc - qc
    cs = np.clip(qc - 8, 0, 48)
    valid = (kc >= cs) & (kc < cs + 16) & (np.abs(dr) <= 7)
    dri = np.clip(dr + 7, 0, 14)
    dci = np.clip(dc + 15, 0, 30)
    dri, dci, valid = np.broadcast_arrays(dri, dci, valid)
    out = np.empty((16, 128, 7, 2, 64), np.float32)
    for h in range(16):
        out[h] = np.where(valid, rel_bias[h][dri, dci], np.float32(NEG))
    out = out.reshape(4, 4, 128, 7, 128).transpose(0, 2, 3, 1, 4)
    return np.ascontiguousarray(out.reshape(4, 128, 7 * 512))


def _rowbias(core):
    p = np.arange(128)[:, None, None, None]
    rp = np.arange(16)[None, :, None, None]
    u = np.arange(7)[None, None, :, None]
    qp = np.arange(2)[None, None, None, :]
    kr = 32 * core + 2 * rp - 6 + 2 * u + p // 64
    qr = 32 * core + 2 * rp + qp
    rs = np.clip(qr - 4, 0, 248)
    valid = (kr >= rs) & (kr < rs + 8)
    return np.ascontiguousarray(np.where(valid, 0.0, NEG).astype(np.float32).reshape(128, 16 * 7 * 2))


_CACHE = {}


def _get_nc(stop_after=None, skip=()):
    key = (stop_after, tuple(skip))
    if key not in _CACHE:
        _CACHE[key] = Builder(stop_after, skip).build()
    return _CACHE[key]


def kernel(x, ffn1_norm, ffn1_w_gate, ffn1_w_up, ffn1_w_down, mix_norm, ffn2_norm, ffn2_w_gate, ffn2_w_up,
           ffn2_w_down, even_w_in, a_q_norm, a_k_norm, b_sink, even_w_out, odd_w_qkv, c_rel_bias, odd_w_out,
           final_norm, _stop_after=None, _skip=()):
    f = lambda a: np.ascontiguousarray(np.asarray(a, dtype=np.float32))
    x = f(x)
    xs = x.reshape(NCORES, T, D)
    gains = f(np.stack([f(ffn1_norm)[0], f(mix_norm)[0], f(ffn2_norm)[0], f(ffn1_norm)[1], f(mix_norm)[1],
                        f(ffn2_norm)[1], f(final_norm)]))
    ga = f(np.concatenate([np.tile(f(a_q_norm)[0], 8), np.tile(f(a_k_norm)[0], 2)])[None, :])
    sinkrow = f(np.repeat(f(b_sink)[0], 128)[None, :])
    rope = _rope_tables()
    mt = _mt_table(f(c_rel_bias)[0])
    common = {
        "gains": gains, "wg1": f(ffn1_w_gate), "wu1": f(ffn1_w_up), "wd1": f(ffn1_w_down),
        "wg2": f(ffn2_w_gate), "wu2": f(ffn2_w_up), "wd2": f(ffn2_w_down),
        "w_in": f(even_w_in)[0], "w_out0": f(even_w_out)[0], "w_qkv": f(odd_w_qkv)[0], "w_out1": f(odd_w_out)[0],
        "ga": ga, "sinkrow": sinkrow, "mt": mt,
    }
    in_maps = []
    for c in range(NCORES):
        m = dict(common)
        m["x"] = np.ascontiguousarray(xs[c])
        m["rope"] = np.ascontiguousarray(rope[c * T:(c + 1) * T])
        m["rowbias"] = _rowbias(c)
        be = np.zeros((128, 2), np.float32)
        if c == 0:
            be[:, 0] = NEG
        if c == NCORES - 1:
            be[:, 1] = NEG
        m["bedge"] = be
        in_maps.append(m)
    if "ffn" in _skip:
        for k in ("wg1", "wu1", "wd1", "wg2", "wu2", "wd2"):
            common[k] = np.zeros((2, 128, 128), np.float32)
        for m in in_maps:
            m.update({k: common[k] for k in ("wg1", "wu1", "wd1", "wg2", "wu2", "wd2")})
    nc = _get_nc(_stop_after, _skip)
    res = run_bass_kernel_spmd(nc, in_maps, core_ids=list(range(NCORES)))
    out = np.concatenate([np.asarray(r["out"], dtype=np.float32) for r in res.results], axis=0)
    return out.reshape(1, NCORES * T, D)
```
